# Optimizing a Trainium2 kernel written in Bass

```python
import math
import jax, jax.numpy as jnp
from jax import lax
import numpy as np

D_MODEL = 1024
BATCH = 2
SEQ = 16384
DEPTH = 2

N_MIXERS = 2
N_ATTN_LAYERS = (DEPTH + N_MIXERS - 1) // N_MIXERS
N_SSM_LAYERS = DEPTH // N_MIXERS

ATTN_HEAD_DIM = 64
ATTN_HEADS = D_MODEL // ATTN_HEAD_DIM
Q_BLOCK = 128
SSM_EXPAND = 2
SSM_D_INNER = SSM_EXPAND * D_MODEL
SSM_HEAD_DIM = 64
SSM_HEADS = SSM_D_INNER // SSM_HEAD_DIM
SSM_GROUPS = 8
SSM_STATE = 128
SSM_CONV = 4
SSM_CHUNK = 128
SSM_XBC = SSM_D_INNER + 2 * SSM_GROUPS * SSM_STATE
FFN_DIM = 2816
FFN_CONV = 3
PLE_DIM = 256

LN_EPS = 1e-5
RMS_EPS = 1e-5
DEEPNORM_ALPHA = (2 * DEPTH) ** 0.25
DEEPNORM_BETA = (8 * DEPTH) ** -0.25

kernel_name = "fox_ssd_interleaved_deepnorm_trunk"


def layer_norm(x, g, b):
    xf = x.astype(jnp.float32)
    mu = jnp.mean(xf, axis=-1, keepdims=True)
    var = jnp.mean(jnp.square(xf - mu), axis=-1, keepdims=True)
    return ((xf - mu) * lax.rsqrt(var + LN_EPS) * g + b).astype(x.dtype)


def causal_dwconv(u, w, b):
    K = w.shape[0]
    S = u.shape[1]
    up = jnp.pad(u, ((0, 0), (K - 1, 0), (0, 0)))
    out = b
    for k in range(K):
        out = out + up[:, k:k + S] * w[k]
    return out


def fox_mixer(x, w_in, b_f, w_out):
    Bsz, S, _ = x.shape
    H, Dh = ATTN_HEADS, ATTN_HEAD_DIM
    proj = x @ w_in
    q, k, v, f_logit = jnp.split(proj, [D_MODEL, 2 * D_MODEL, 3 * D_MODEL], axis=-1)
    q = q.reshape(Bsz, S, H, Dh).transpose(0, 2, 1, 3)
    k = k.reshape(Bsz, S, H, Dh).transpose(0, 2, 1, 3)
    v = v.reshape(Bsz, S, H, Dh).transpose(0, 2, 1, 3)
    log_f = jax.nn.log_sigmoid(f_logit.astype(jnp.float32) + b_f)
    c = jnp.cumsum(log_f, axis=1).transpose(0, 2, 1)
    nb = S // Q_BLOCK
    qb = q.reshape(Bsz, H, nb, Q_BLOCK, Dh).transpose(2, 0, 1, 3, 4)
    cqb = c.reshape(Bsz, H, nb, Q_BLOCK).transpose(2, 0, 1, 3)
    kpos = jnp.arange(S)
    scale = 1.0 / math.sqrt(Dh)

    def block(args):
        qi, cqi, bi = args
        s = jnp.einsum('bhqd,bhkd->bhqk', qi, k).astype(jnp.float32) * scale
        s = s + cqi[..., None] - c[:, :, None, :]
        qpos = bi * Q_BLOCK + jnp.arange(Q_BLOCK)
        s = jnp.where(kpos[None, :] <= qpos[:, None], s, -jnp.inf)
        pr = jax.nn.softmax(s, axis=-1)
        return jnp.einsum('bhqk,bhkd->bhqd', pr.astype(v.dtype), v)

    o = lax.map(block, (qb, cqb, jnp.arange(nb)))
    o = o.transpose(1, 0, 3, 2, 4).reshape(Bsz, S, D_MODEL)
    return o @ w_out


def ssd_scan(xh, dt, A, Bm, Cm):
    Bsz, S = xh.shape[0], xh.shape[1]
    G, R, P, N, Q = SSM_GROUPS, SSM_HEADS // SSM_GROUPS, SSM_HEAD_DIM, SSM_STATE, SSM_CHUNK
    nc = S // Q

    def to_chunks(t):
        return jnp.moveaxis(t.reshape((Bsz, nc, Q) + t.shape[2:]), 1, 0)

    xc = to_chunks(xh.reshape(Bsz, S, G, R, P))
    dtc = to_chunks(dt.reshape(Bsz, S, G, R))
    Bc = to_chunks(Bm)
    Cc = to_chunks(Cm)
    A_g = A.reshape(G, R)
    tri = jnp.tril(jnp.ones((Q, Q), dtype=bool))[None, :, :, None, None]

    def step(state, inp):
        x_, dt_, B_, C_ = inp
        acum = jnp.cumsum(dt_ * A_g, axis=1)
        seg = acum[:, :, None] - acum[:, None, :]
        L = jnp.exp(jnp.where(tri, seg, -jnp.inf))
        CB = jnp.einsum('btgn,bsgn->btsg', C_, B_)
        y_intra = jnp.einsum('btsg,btsgr,bsgr,bsgrp->btgrp', CB, L, dt_, x_)
        y_inter = jnp.einsum('btgn,bgrpn,btgr->btgrp', C_, state, jnp.exp(acum))
        w_end = jnp.exp(acum[:, -1:] - acum) * dt_
        new_state = state * jnp.exp(acum[:, -1])[..., None, None] + \
            jnp.einsum('bsgn,bsgr,bsgrp->bgrpn', B_, w_end, x_)
        return new_state, (y_intra + y_inter).astype(jnp.float32)

    state0 = jnp.zeros((Bsz, G, R, P, N), jnp.float32)
    _, y = lax.scan(step, state0, (xc, dtc, Bc, Cc))
    return jnp.moveaxis(y, 0, 1).reshape(Bsz, S, SSM_HEADS, P)


def ssd_mixer(x, w_in, conv_w, conv_b, dt_bias, A_log, D_skip, norm_w, w_out):
    Bsz, S, _ = x.shape
    GN = SSM_GROUPS * SSM_STATE
    proj = x @ w_in
    z, xBC, dt_raw = jnp.split(proj, [SSM_D_INNER, SSM_D_INNER + SSM_XBC], axis=-1)
    xBC = jax.nn.silu(causal_dwconv(xBC, conv_w, conv_b))
    xs, Bm, Cm = jnp.split(xBC, [SSM_D_INNER, SSM_D_INNER + GN], axis=-1)
    dt = jax.nn.softplus(dt_raw.astype(jnp.float32) + dt_bias)
    A = -jnp.exp(A_log.astype(jnp.float32))
    xh = xs.reshape(Bsz, S, SSM_HEADS, SSM_HEAD_DIM)
    y = ssd_scan(xh, dt, A,
                 Bm.reshape(Bsz, S, SSM_GROUPS, SSM_STATE),
                 Cm.reshape(Bsz, S, SSM_GROUPS, SSM_STATE))
    y = y + D_skip[:, None] * xh
    y = y.reshape(Bsz, S, SSM_D_INNER) * jax.nn.silu(z.astype(jnp.float32))
    yg = y.reshape(Bsz, S, SSM_GROUPS, SSM_D_INNER // SSM_GROUPS)
    yg = yg * lax.rsqrt(jnp.mean(jnp.square(yg), axis=-1, keepdims=True) + RMS_EPS)
    y = yg.reshape(Bsz, S, SSM_D_INNER) * norm_w
    return y.astype(x.dtype) @ w_out


def conv_ffn(x, w_up, conv_w, conv_b, w_down):
    u, g = jnp.split(x @ w_up, [FFN_DIM], axis=-1)
    g = causal_dwconv(g, conv_w, conv_b)
    return (jax.nn.gelu(g, approximate=False) * u) @ w_down


def setup_inputs(seed: int = 0) -> dict:
    key = jax.random.key(seed)
    ks = iter(jax.random.split(key, 40))

    def nrm(shape, scale):
        return jax.random.normal(next(ks), shape, jnp.float32) * scale

    D, H = D_MODEL, ATTN_HEADS
    NA, NB = N_ATTN_LAYERS, N_SSM_LAYERS
    beta = DEEPNORM_BETA
    x = nrm((BATCH, SEQ, D), 1.0)
    p = nrm((DEPTH, BATCH, SEQ, PLE_DIM), 1.0)
    attn_w_in = nrm((NA, D, 3 * D + H), D ** -0.5)
    attn_w_in = attn_w_in.at[:, :, 2 * D:3 * D].multiply(beta)
    attn_b_f = jax.random.uniform(next(ks), (NA, H), jnp.float32, 1.0, 6.0)
    attn_w_out = nrm((NA, D, D), beta * D ** -0.5)
    ssm_in_dim = 2 * SSM_D_INNER + 2 * SSM_GROUPS * SSM_STATE + SSM_HEADS
    ssm_w_in = nrm((NB, D, ssm_in_dim), D ** -0.5)
    ssm_conv_w = nrm((NB, SSM_CONV, SSM_XBC), SSM_CONV ** -0.5)
    ssm_conv_b = nrm((NB, SSM_XBC), 0.02)
    dt0 = jnp.exp(jax.random.uniform(next(ks), (NB, SSM_HEADS), jnp.float32,
                                     math.log(1e-3), math.log(1e-1)))
    ssm_dt_bias = dt0 + jnp.log(-jnp.expm1(-dt0))
    ssm_A_log = jnp.log(jax.random.uniform(next(ks), (NB, SSM_HEADS), jnp.float32, 1.0, 16.0))
    ssm_D = 1.0 + nrm((NB, SSM_HEADS), 0.1)
    ssm_norm_w = 1.0 + nrm((NB, SSM_D_INNER), 0.05)
    ssm_w_out = nrm((NB, SSM_D_INNER, D), beta * SSM_D_INNER ** -0.5)
    ln_mix_g = 1.0 + nrm((DEPTH, D), 0.05)
    ln_mix_b = nrm((DEPTH, D), 0.02)
    ffn_w_up = nrm((DEPTH, D, 2 * FFN_DIM), D ** -0.5)
    ffn_conv_w = nrm((DEPTH, FFN_CONV, FFN_DIM), FFN_CONV ** -0.5)
    ffn_conv_b = nrm((DEPTH, FFN_DIM), 0.02)
    ffn_w_down = nrm((DEPTH, FFN_DIM, D), beta * FFN_DIM ** -0.5)
    ln_ffn_g = 1.0 + nrm((DEPTH, D), 0.05)
    ln_ffn_b = nrm((DEPTH, D), 0.02)
    ple_w_proj = nrm((DEPTH, PLE_DIM, D), beta * PLE_DIM ** -0.5)
    ple_w_gate = nrm((DEPTH, D, D), D ** -0.5)
    ple_b_gate = nrm((DEPTH, D), 0.02)
    return {"x": x, "p": p,
            "attn_w_in": attn_w_in, "attn_b_f": attn_b_f, "attn_w_out": attn_w_out,
            "ssm_w_in": ssm_w_in, "ssm_conv_w": ssm_conv_w, "ssm_conv_b": ssm_conv_b,
            "ssm_dt_bias": ssm_dt_bias, "ssm_A_log": ssm_A_log, "ssm_D": ssm_D,
            "ssm_norm_w": ssm_norm_w, "ssm_w_out": ssm_w_out,
            "ln_mix_g": ln_mix_g, "ln_mix_b": ln_mix_b,
            "ffn_w_up": ffn_w_up, "ffn_conv_w": ffn_conv_w, "ffn_conv_b": ffn_conv_b,
            "ffn_w_down": ffn_w_down, "ln_ffn_g": ln_ffn_g, "ln_ffn_b": ln_ffn_b,
            "ple_w_proj": ple_w_proj, "ple_w_gate": ple_w_gate, "ple_b_gate": ple_b_gate}


def reference(x, p, attn_w_in, attn_b_f, attn_w_out, ssm_w_in, ssm_conv_w, ssm_conv_b,
              ssm_dt_bias, ssm_A_log, ssm_D, ssm_norm_w, ssm_w_out, ln_mix_g, ln_mix_b,
              ffn_w_up, ffn_conv_w, ffn_conv_b, ffn_w_down, ln_ffn_g, ln_ffn_b,
              ple_w_proj, ple_w_gate, ple_b_gate):
    for i in range(DEPTH):
        j = i // N_MIXERS
        if i % N_MIXERS == 0:
            mix = fox_mixer(x, attn_w_in[j], attn_b_f[j], attn_w_out[j])
        else:
            mix = ssd_mixer(x, ssm_w_in[j], ssm_conv_w[j], ssm_conv_b[j], ssm_dt_bias[j],
                            ssm_A_log[j], ssm_D[j], ssm_norm_w[j], ssm_w_out[j])
        x = layer_norm(DEEPNORM_ALPHA * x + mix, ln_mix_g[i], ln_mix_b[i])
        ffn = conv_ffn(x, ffn_w_up[i], ffn_conv_w[i], ffn_conv_b[i], ffn_w_down[i])
        x = layer_norm(DEEPNORM_ALPHA * x + ffn, ln_ffn_g[i], ln_ffn_b[i])
        gate = jax.nn.sigmoid(x @ ple_w_gate[i] + ple_b_gate[i])
        x = x + gate * (p[i] @ ple_w_proj[i])
    return x
```

```python
import contextlib
import math
import numpy as np
import concourse.bass as bass
import concourse.mybir as mybir
from concourse.bass_utils import run_bass_kernel_spmd

F32 = mybir.dt.float32
BF16 = mybir.dt.bfloat16
AF = mybir.ActivationFunctionType
ALU = mybir.AluOpType
AX = mybir.AxisListType

ENGS = ("pe", "act", "dve", "pool", "sp")

D_MODEL = 1024
FFN = 2816
NFC = FFN // 128
ALPHA = 4 ** 0.25
LN_EPS = 1e-5
RMS_EPS = 1e-5


class Prog:
    def __init__(self, nc):
        self.nc = nc
        self.ops = {e: [] for e in ENGS}
        self.res = {}
        self.dma_sems = {}

    def _deps(self, eng, reads, writes):
        deps = []
        for k in reads:
            r = self.res.get(k)
            if r and r[0] is not None:
                deps.append(r[0])
        for k in writes:
            r = self.res.get(k)
            if r:
                if r[0] is not None:
                    deps.append(r[0])
                deps.extend(r[1])
        if eng == "pe":
            deps = [d for d in deps if not (d[0] == "c" and d[1] == "pe")]
        return deps

    def _commit(self, tok, reads, writes):
        for k in reads:
            r = self.res.setdefault(k, [None, []])
            r[1].append(tok)
        for k in writes:
            self.res[k] = [tok, []]

    def op(self, eng, fn, reads=(), writes=()):
        deps = self._deps(eng, reads, writes)
        tok = ("c", eng, len(self.ops[eng]))
        self.ops[eng].append({"fn": fn, "deps": deps, "tok": tok, "sig": False})
        self._commit(tok, reads, writes)
        return tok

    def dma(self, fn, sem, reads=(), writes=(), n=1, eng="sp"):
        deps = self._deps(eng, reads, writes)
        cnt = self.dma_sems.get(sem, 0) + n
        self.dma_sems[sem] = cnt
        tok = ("d", sem, 16 * cnt)
        self.ops[eng].append({"fn": fn, "deps": deps, "tok": tok, "sig": True, "dma": sem})
        self._commit(tok, reads, writes)
        return tok

    def mm(self, out, lhsT, rhs, start, stop, r, w):
        return self.op("pe", lambda e: e.matmul(out, lhsT=lhsT, rhs=rhs, start=start, stop=stop), r, w)

    def tr(self, out, in_, ident, r, w):
        return self.op("pe", lambda e: e.transpose(out, in_, ident), r, w)

    def act(self, out, in_, func, r, w, bias=None, scale=None):
        kw = {}
        if bias is not None:
            kw["bias"] = bias
        if scale is not None:
            kw["scale"] = scale
        return self.op("act", lambda e: e.activation(out=out, in_=in_, func=func, **kw), r, w)

    def cp(self, eng, out, in_, r, w):
        if eng == "act":
            return self.op("act", lambda e: e.copy(out=out, in_=in_), r, w)
        return self.op(eng, lambda e: e.tensor_copy(out=out, in_=in_), r, w)

    def tt(self, eng, out, in0, in1, op, r, w):
        return self.op(eng, lambda e: e.tensor_tensor(out=out, in0=in0, in1=in1, op=op), r, w)

    def ts(self, eng, out, in0, s1, s2, op0, op1, r, w):
        if s2 is None:
            return self.op(eng, lambda e: e.tensor_scalar(out=out, in0=in0, scalar1=s1, scalar2=None, op0=op0), r, w)
        return self.op(eng, lambda e: e.tensor_scalar(out=out, in0=in0, scalar1=s1, scalar2=s2, op0=op0, op1=op1), r, w)

    def stt(self, eng, out, in0, scalar, in1, op0, op1, r, w):
        return self.op(eng, lambda e: e.scalar_tensor_tensor(out=out, in0=in0, scalar=scalar, in1=in1, op0=op0, op1=op1), r, w)

    def memset(self, eng, ap, val, w):
        return self.op(eng, lambda e: e.memset(ap, val), (), w)

    def ld(self, out, in_, sem, r, w):
        return self.dma(lambda e: e.dma_start(out=out, in_=in_), sem, r, w)

    def emit(self, final_waits=()):
        nc = self.nc
        for e in ENGS:
            for o in self.ops[e]:
                for d in o["deps"]:
                    if d[0] == "c":
                        self.ops[d[1]][d[2]]["sig"] = True
        for t in final_waits:
            if t[0] == "c":
                self.ops[t[1]][t[2]]["sig"] = True
        cnt_of = {}
        for e in ENGS:
            c = 0
            for i, o in enumerate(self.ops[e]):
                if o["tok"][0] == "c" and o["sig"]:
                    c += 1
                    cnt_of[(e, i)] = c
        with contextlib.ExitStack() as st:
            esem = {e: st.enter_context(nc.semaphore("s_" + e)) for e in ENGS if e != "sp"}
            dsem = {n: st.enter_context(nc.semaphore("d_" + n)) for n in self.dma_sems}
            block = st.enter_context(nc.Block())

            def resolve(tok):
                if tok[0] == "c":
                    return esem[tok[1]], cnt_of[(tok[1], tok[2])], ("c", tok[1])
                return dsem[tok[1]], tok[2], ("d", tok[1])

            def run(e, engobj):
                waited = {}
                for o in self.ops[e]:
                    need = {}
                    for d in o["deps"]:
                        s, v, key = resolve(d)
                        if waited.get(key, 0) >= v:
                            continue
                        if key not in need or need[key][1] < v:
                            need[key] = (s, v)
                    for key, (s, v) in need.items():
                        engobj.wait_ge(s, v)
                        waited[key] = v
                    if "dma" in o:
                        instrs = o["fn"](engobj)
                        if not isinstance(instrs, (list, tuple)):
                            instrs = [instrs]
                        for ins in instrs:
                            ins.then_inc(dsem[o["dma"]], 16)
                    else:
                        ins = o["fn"](engobj)
                        if o["sig"]:
                            ins.then_inc(esem[e], 1)
                if e == "sp":
                    for t in final_waits:
                        s, v, key = resolve(t)
                        if waited.get(key, 0) < v:
                            engobj.wait_ge(s, v)
                            waited[key] = v

            @block.tensor
            def _(eng):
                run("pe", eng)

            @block.scalar
            def _(eng):
                run("act", eng)

            @block.vector
            def _(eng):
                run("dve", eng)

            @block.gpsimd
            def _(eng):
                run("pool", eng)

            @block.sync
            def _(eng):
                run("sp", eng)


class Ring:
    def __init__(self, tiles, name):
        self.tiles = tiles
        self.name = name
        self.i = 0

    def next(self):
        j = self.i % len(self.tiles)
        self.i += 1
        return self.tiles[j], "%s%d" % (self.name, j)


class Ctx:
    def __init__(self, nc, st):
        self.nc = nc
        self.st = st

    def sb(self, name, shape, dt=F32):
        return self.st.enter_context(self.nc.sbuf_tensor(name, shape, dt))

    def ps(self, name, shape, dt=F32):
        return self.st.enter_context(self.nc.psum_tensor(name, shape, dt))

    def ring(self, name, n, shape, dt=F32):
        return Ring([self.sb("%s_%d" % (name, i), shape, dt) for i in range(n)], name)


def _consts(P, C, need_f32=True):
    ident = C.sb("c_ident", [128, 128]); tri = C.sb("c_tri", [128, 128]); upp = C.sb("c_upp", [128, 128])
    ones = C.sb("c_ones", [128, 128])
    P.memset("pool", ident[:], 1.0, ["ident"])
    P.op("pool", lambda e: e.affine_select(out=ident[:], in_=ident[:], pattern=[[-1, 128]], compare_op=ALU.is_equal,
                                            fill=0.0, base=0, channel_multiplier=1), ["ident"], ["ident"])
    P.memset("pool", tri[:], 1.0, ["tri"])
    P.op("pool", lambda e: e.affine_select(out=tri[:], in_=tri[:], pattern=[[1, 128]], compare_op=ALU.is_ge,
                                            fill=0.0, base=0, channel_multiplier=-1), ["tri"], ["tri"])
    P.memset("pool", upp[:], 1.0, ["upp"])
    P.op("pool", lambda e: e.affine_select(out=upp[:], in_=upp[:], pattern=[[-1, 128]], compare_op=ALU.is_gt,
                                            fill=0.0, base=0, channel_multiplier=1), ["upp"], ["upp"])
    P.memset("pool", ones[:], 1.0, ["ones"])
    return ident, tri, upp, ones


def build_attn(S, NH=4):
    nc = bass.Bass("TRN2", target_bir_lowering=False)
    NT = min(512, S)
    ntile = S // NT
    nblk_t = NT // 128
    NBLK = S // 128
    xT = nc.dram_tensor("xT", [1024, S], F32, kind="ExternalInput").ap()
    wq = nc.dram_tensor("wq", [1024, NH * 64], F32, kind="ExternalInput").ap()
    wk = nc.dram_tensor("wk", [1024, NH * 64], F32, kind="ExternalInput").ap()
    wv = nc.dram_tensor("wv", [1024, NH * 64], F32, kind="ExternalInput").ap()
    wf = nc.dram_tensor("wf", [1024, NH], F32, kind="ExternalInput").ap()
    bfd = nc.dram_tensor("bf", [1, NH], F32, kind="ExternalInput").ap()
    oT = nc.dram_tensor("oT", [NH * 64, S], BF16, kind="ExternalOutput").ap()
    xTv = xT.rearrange("(kc p) s -> p kc s", p=128)
    P = Prog(nc)
    with contextlib.ExitStack() as st:
        C = Ctx(nc, st)
        wstage = C.sb("wstage", [128, 8, NH * 64])
        wfs = C.sb("wfs", [128, 8, NH])
        Wq = C.sb("Wq", [128, 8, NH, 70], BF16); Wk = C.sb("Wk", [128, 8, NH, 70], BF16)
        Wv = C.sb("Wv", [128, 8, NH, 64], BF16); Wf = C.sb("Wf", [128, 8, NH], BF16)
        bfs = C.sb("bfs", [1, NH]); nbf = C.sb("nbf", [1, NH])
        sels = C.sb("sels", [1, 8, 70], BF16); onesrow = C.sb("onesrow", [1, 512], BF16); onesf = C.sb("onesf", [1, 512])
        maskneg = C.sb("maskneg", [128, 128])
        KT = C.sb("KT", [70, S], BF16)
        Vp = C.sb("Vp", [128, NBLK, 128], BF16)
        xs_r = C.ring("xs", 2, [128, 8, NT]); xb_r = C.ring("xb", 2, [128, 8, NT], BF16)
        QT_r = C.ring("QT", 2, [70, NT], BF16)
        PT_r = C.ring("PT", 4, [128, NT], BF16)
        sd_r = C.ring("sd", 2, [128, 128])
        ob_r = C.ring("ob", 2, [64, NT], BF16)
        rden = C.sb("rden", [128, NT]); rd0 = C.sb("rd0", [64, NT])
        fe = C.sb("fe", [1, NT]); fl = C.sb("fl", [1, NT]); ncum_r = C.ring("ncum", 2, [1, NT])
        cprev = C.sb("cprev", [1, 1])
        r1 = C.sb("r1", [1, NT]); r2 = C.sb("r2", [1, NT])
        spl_r = C.ring("spl", 2, [1, 6, NT], BF16)
        banks = [C.ps("bank%d" % i, [128, 512]) for i in range(8)]
        s_r = Ring(banks[0:3], "psS"); o_r = Ring(banks[3:5], "psO"); m_r = Ring(banks[5:8], "psM")

        P.memset("pool", Wq[:], 0.0, ["Wq"]); P.memset("pool", Wk[:], 0.0, ["Wk"])
        for (src, dst, key, scale) in ((wq, Wq, "Wq", 0.125), (wk, Wk, "Wk", None), (wv, Wv, "Wv", None)):
            P.ld(wstage[:], src.rearrange("(kc p) n -> p kc n", p=128), "ldw", [], ["wstage"])
            for h in range(NH):
                if scale is not None:
                    P.ts("dve", dst[:, :, h, 0:64], wstage[:, :, h * 64:(h + 1) * 64], scale, None, ALU.mult, None, ["wstage"], [key])
                else:
                    P.cp("dve", dst[:, :, h, 0:64], wstage[:, :, h * 64:(h + 1) * 64], ["wstage"], [key])
        P.ld(wfs[:], wf.rearrange("(kc p) n -> p kc n", p=128), "ldwf", [], ["wfs"])
        P.cp("dve", Wf[:], wfs[:], ["wfs"], ["Wf"])
        P.ld(bfs[:], bfd[:, :], "ldbf", [], ["bfs"])
        P.ts("dve", nbf[:], bfs[:], -1.0, None, ALU.mult, None, ["bfs"], ["nbf"])
        P.memset("pool", sels[:], 0.0, ["sels"])
        for j in range(3):
            P.memset("pool", sels[0:1, j, 64 + j:65 + j], 1.0, ["sels"])
            P.memset("pool", sels[0:1, 4 + j, 67 + j:68 + j], 1.0, ["sels"])
        P.memset("pool", sels[0:1, 3, 67:70], 1.0, ["sels"])
        P.memset("pool", sels[0:1, 7, 64:67], 1.0, ["sels"])
        P.memset("pool", onesrow[:], 1.0, ["onesrow"]); P.memset("pool", onesf[:], 1.0, ["onesf"])
        P.memset("pool", maskneg[:], 0.0, ["maskneg"])
        P.op("pool", lambda e: e.affine_select(out=maskneg[:], in_=maskneg[:], pattern=[[1, 128]], compare_op=ALU.is_ge,
                                                fill=-30000.0, base=0, channel_multiplier=-1), ["maskneg"], ["maskneg"])
        P.memset("pool", Vp[:, :, 64:128], 1.0, ["Vones"])

        last_store = None

        def fchain(h, t, xb, xbk):
            n = NT
            pf, pfk = m_r.next()
            for kc in range(8):
                P.mm(pf[0:1, 0:n], Wf[:, kc, h:h + 1], xb[:, kc, :], kc == 0, kc == 7, ["Wf", xbk], [pfk])
            P.act(fe[:], pf[0:1, 0:n], AF.Exp, [pfk, "nbf"], ["fe"], bias=nbf[0:1, h:h + 1], scale=-1.0)
            P.act(fl[:], fe[:], AF.Ln, ["fe"], ["fl"], bias=1.0)
            ncum, nck = ncum_r.next()
            if t == 0:
                P.op("dve", lambda e: e.tensor_tensor_scan(out=ncum[:], data0=onesf[0:1, 0:n], data1=fl[:], initial=0.0,
                                                           op0=ALU.mult, op1=ALU.add), ["fl", "onesf"], [nck])
            else:
                P.op("dve", lambda e: e.tensor_tensor_scan(out=ncum[:], data0=onesf[0:1, 0:n], data1=fl[:], initial=cprev[0:1, 0:1],
                                                           op0=ALU.mult, op1=ALU.add), ["fl", "onesf", "cprev"], [nck])
            P.cp("dve", cprev[:], ncum[0:1, n - 1:n], [nck], ["cprev"])
            spl, splk = spl_r.next()
            P.cp("dve", spl[0:1, 0, :], ncum[:], [nck], [splk])
            P.tt("dve", r1[:], ncum[:], spl[0:1, 0, :], ALU.subtract, [nck, splk], ["r1"])
            P.cp("dve", spl[0:1, 1, :], r1[:], ["r1"], [splk])
            P.tt("dve", r2[:], r1[:], spl[0:1, 1, :], ALU.subtract, ["r1", splk], ["r2"])
            P.cp("dve", spl[0:1, 2, :], r2[:], ["r2"], [splk])
            P.ts("dve", spl[0:1, 3:6, :], spl[0:1, 0:3, :], -1.0, None, ALU.mult, None, [splk], [splk])
            return spl, splk

        def load_x(t):
            xs, xsk = xs_r.next()
            P.ld(xs[:], xTv[:, :, t * NT:(t + 1) * NT], xsk, [], [xsk])
            xb, xbk = xb_r.next()
            P.cp("pool", xb[:], xs[:], [xsk], [xbk])
            return xb, xbk

        for h in range(NH):
            xb, xbk = load_x(0)
            spl, splk = fchain(h, 0, xb, xbk)
            for t in range(ntile):
                c0 = t * NT
                n = NT
                pq, pqk = m_r.next()
                for kc in range(8):
                    P.mm(pq[0:70, 0:n], Wq[:, kc, h, :], xb[:, kc, :], kc == 0, False, ["Wq", xbk], [pqk])
                for j in range(3):
                    P.mm(pq[0:70, 0:n], sels[0:1, j, :], spl[0:1, 3 + j, :], False, False, ["sels", splk], [pqk])
                P.mm(pq[0:70, 0:n], sels[0:1, 3, :], onesrow[0:1, 0:n], False, True, ["sels", "onesrow"], [pqk])
                QT, QTk = QT_r.next()
                P.cp("dve", QT[:], pq[0:70, 0:n], [pqk], [QTk])
                pk, pkk = m_r.next()
                for kc in range(8):
                    P.mm(pk[0:70, 0:n], Wk[:, kc, h, :], xb[:, kc, :], kc == 0, False, ["Wk", xbk], [pkk])
                for j in range(3):
                    P.mm(pk[0:70, 0:n], sels[0:1, 4 + j, :], spl[0:1, j, :], False, False, ["sels", splk], [pkk])
                P.mm(pk[0:70, 0:n], sels[0:1, 7, :], onesrow[0:1, 0:n], False, True, ["sels", "onesrow"], [pkk])
                P.cp("dve", KT[:, c0:c0 + n], pk[0:70, 0:n], [pkk], ["KT%d" % t])
                pv, pvk = m_r.next()
                for i in range(nblk_t):
                    for kc in range(8):
                        P.mm(pv[:, i * 64:(i + 1) * 64], xb[:, kc, i * 128:(i + 1) * 128], Wv[:, kc, h, :], kc == 0, kc == 7,
                             ["Wv", xbk], [pvk])
                for i in range(nblk_t):
                    P.cp("dve", Vp[:, t * nblk_t + i, 0:64], pv[:, i * 64:(i + 1) * 64], [pvk], ["V%d" % t])
                if t + 1 < ntile:
                    xb_n, xbk_n = load_x(t + 1)
                    spl_n, splk_n = fchain(h, t + 1, xb_n, xbk_n)
                po, pok = o_r.next()
                nkb = (t + 1) * nblk_t
                for j in range(nkb):
                    kt = j // nblk_t
                    ps_, psk = s_r.next()
                    PT, PTk = PT_r.next()
                    if j < t * nblk_t:
                        lo = 0
                        P.mm(ps_[:, 0:n], KT[:, j * 128:(j + 1) * 128], QT[:], True, True, ["KT%d" % kt, QTk], [psk])
                        P.act(PT[:, 0:n], ps_[:, 0:n], AF.Exp, [psk], [PTk])
                    else:
                        i = j - t * nblk_t
                        lo = i * 128
                        P.mm(ps_[:, lo:n], KT[:, j * 128:(j + 1) * 128], QT[:, lo:n], True, True, ["KT%d" % kt, QTk], [psk])
                        sd, sdk = sd_r.next()
                        P.tt("dve", sd[:], ps_[:, lo:lo + 128], maskneg[:], ALU.add, [psk, "maskneg"], [sdk])
                        P.act(PT[:, lo:lo + 128], sd[:], AF.Exp, [sdk], [PTk])
                        if lo + 128 < n:
                            P.act(PT[:, lo + 128:n], ps_[:, lo + 128:n], AF.Exp, [psk, sdk], [PTk])
                    P.mm(po[:, lo:n], Vp[:, j, :], PT[:, lo:n], j == 0, j == nkb - 1, ["V%d" % kt, "Vones", PTk], [pok])
                P.op("dve", lambda e, po=po: e.reciprocal(out=rden[64:128, 0:n], in_=po[64:128, 0:n]), [pok], ["rden"])
                P.cp("act", rd0[0:64, 0:n], rden[64:128, 0:n], ["rden"], ["rd0"])
                ob, obk = ob_r.next()
                P.tt("dve", ob[:, 0:n], po[0:64, 0:n], rd0[0:64, 0:n], ALU.mult, [pok, "rd0"], [obk])
                last_store = P.ld(oT[h * 64:(h + 1) * 64, c0:c0 + n], ob[:, 0:n], obk + "st", [obk], [])
                if t + 1 < ntile:
                    xb, xbk, spl, splk = xb_n, xbk_n, spl_n, splk_n
        P.emit(final_waits=[last_store] + [("d", k, 16 * v) for k, v in P.dma_sems.items() if k.endswith("st")])
    return nc


VEC_LN1G, VEC_LN1B, VEC_LN2G, VEC_LN2B, VEC_BG, VEC_CB, VEC_CW = 0, 8, 16, 24, 32, 40, 62
NVEC = 128


def build_post(Kin, T, NT):
    nc = bass.Bass("TRN2", target_bir_lowering=False)
    KC = Kin // 128
    ntile = T // NT
    mT = nc.dram_tensor("mT", [Kin, 2 + T], BF16, kind="ExternalInput").ap()
    xT = nc.dram_tensor("xT", [1024, 2 + T], F32, kind="ExternalInput").ap()
    pT = nc.dram_tensor("pT", [256, T], F32, kind="ExternalInput").ap()
    hmask = nc.dram_tensor("hmask", [128, 1], F32, kind="ExternalInput").ap()
    vecs = nc.dram_tensor("vecs", [128, NVEC], F32, kind="ExternalInput").ap()
    w_out = nc.dram_tensor("w_out", [Kin, 1024], F32, kind="ExternalInput").ap()
    w_up = nc.dram_tensor("w_up", [1024, 2 * FFN], F32, kind="ExternalInput").ap()
    w_down = nc.dram_tensor("w_down", [FFN, 1024], F32, kind="ExternalInput").ap()
    w_gate = nc.dram_tensor("w_gate", [1024, 1024], F32, kind="ExternalInput").ap()
    w_proj = nc.dram_tensor("w_proj", [256, 1024], F32, kind="ExternalInput").ap()
    yT = nc.dram_tensor("yT", [1024, T], F32, kind="ExternalOutput").ap()
    ybT = nc.dram_tensor("ybT", [1024, T], BF16, kind="ExternalOutput").ap()
    mTv = mT.rearrange("(kc p) s -> p kc s", p=128)
    xTv = xT.rearrange("(kc p) s -> p kc s", p=128)
    pTv = pT.rearrange("(kc p) s -> p kc s", p=128)
    yTv = yT.rearrange("(kc p) s -> p kc s", p=128)
    ybTv = ybT.rearrange("(kc p) s -> p kc s", p=128)
    w_out_v = w_out.rearrange("(kc p) n -> p kc n", p=128)
    w_up_v = w_up.rearrange("(kc p) n -> p kc n", p=128)
    w_down_v = w_down.rearrange("(kc p) n -> p kc n", p=128)
    w_gate_v = w_gate.rearrange("(kc p) n -> p kc n", p=128)
    w_proj_v = w_proj.rearrange("(kc p) n -> p kc n", p=128)
    P = Prog(nc)
    with contextlib.ExitStack() as st:
        C = Ctx(nc, st)
        V = C.sb("vecs_sb", [128, NVEC]); hm = C.sb("hm", [128, 1])
        onesb = C.sb("onesb", [128, 128], BF16)
        bigb = C.sb("bigb", [128, max(KC, NFC), NT], BF16)
        xres = C.sb("xres", [128, 8, NT]); xm = C.sb("xm", [128, 8, NT])
        actb = C.sb("actb", [128, 8, NT], BF16); outb = C.sb("outb", [128, 8, NT], BF16)
        rb_r = C.ring("rb", 2, [128, NT], BF16); sq_r = C.ring("sq", 2, [128, NT], BF16)
        mean = C.sb("mean", [128, NT]); var = C.sb("var", [128, NT]); rstd = C.sb("rstd", [128, NT]); nmr = C.sb("nmr", [128, NT])
        tmp_r = C.ring("tmp", 3, [128, NT])
        u_r = C.ring("u", 3, [128, NT], BF16); g_r = C.ring("g", 3, [128, NT + 2]); acc_r = C.ring("acc", 3, [128, NT])
        hg_r = C.ring("hg", 2, [128, NT]); gate_r = C.ring("gate", 2, [128, NT])
        ghalo = C.sb("ghalo", [128, NFC, 2])
        pst = C.sb("pst", [128, 2, NT]); pb = C.sb("pb", [128, 2, NT], BF16)
        WS = 4096
        wpb = C.sb("wpb", [128, 2, 1024], BF16)
        ws_r = C.ring("ws", 2, [128, WS]); wb_r = C.ring("wb", 2, [128, WS], BF16)
        banks = [C.ps("bank%d" % i, [128, 512]) for i in range(8)]
        ps_r = Ring(banks[0:6], "psA"); st_r = Ring(banks[6:8], "psT")

        P.ld(V[:], vecs[:, :], "ldc", [], ["V"])
        P.ld(hm[:], hmask[:, :], "ldhm", [], ["hm"])
        P.memset("pool", onesb[:], 1.0, ["onesb"])
        P.memset("pool", ghalo[:], 0.0, ["ghalo%d" % fc for fc in range(NFC)])

        def stream_w(view, a, b):
            ws, wsk = ws_r.next(); wb, wbk = wb_r.next()
            wsv = ws[:, 0:a * b].rearrange("p (a b) -> p a b", b=b)
            wbv = wb[:, 0:a * b].rearrange("p (a b) -> p a b", b=b)
            P.ld(wsv, view, wsk, [], [wsk])
            P.cp("pool", wbv, wsv, [wsk], [wbk])
            return wbv, wbk

        def layernorm(src, srck, gcol, bcol, dst, dstk, dstb, dstbk, n):
            pss, pssk = st_r.next(); psq, psqk = st_r.next()
            for oc in range(8):
                rb, rbk = rb_r.next(); sq, sqk = sq_r.next()
                P.cp("pool", rb[:, 0:n], src[:, oc, 0:n], [srck], [rbk])
                P.act(sq[:, 0:n], src[:, oc, 0:n], AF.Square, [srck], [sqk])
                P.mm(pss[:, 0:n], onesb[:], rb[:, 0:n], oc == 0, oc == 7, ["onesb", rbk], [pssk])
                P.mm(psq[:, 0:n], onesb[:], sq[:, 0:n], oc == 0, oc == 7, ["onesb", sqk], [psqk])
            P.ts("dve", mean[:, 0:n], pss[:, 0:n], 1.0 / 1024, None, ALU.mult, None, [pssk], ["mean"])
            P.tt("dve", nmr[:, 0:n], mean[:, 0:n], mean[:, 0:n], ALU.mult, ["mean"], ["nmr"])
            P.stt("dve", var[:, 0:n], psq[:, 0:n], 1.0 / 1024, nmr[:, 0:n], ALU.mult, ALU.subtract, [psqk, "nmr"], ["var"])
            P.act(var[:, 0:n], var[:, 0:n], AF.Sqrt, ["var"], ["var"], bias=LN_EPS)
            P.op("dve", lambda e: e.reciprocal(out=rstd[:, 0:n], in_=var[:, 0:n]), ["var"], ["rstd"])
            P.stt("dve", nmr[:, 0:n], mean[:, 0:n], -1.0, rstd[:, 0:n], ALU.mult, ALU.mult, ["mean", "rstd"], ["nmr"])
            for oc in range(8):
                t1, t1k = tmp_r.next()
                e1 = "dve" if oc % 2 == 0 else "pool"
                P.tt(e1, t1[:, 0:n], src[:, oc, 0:n], rstd[:, 0:n], ALU.mult, [srck, "rstd"], [t1k])
                P.tt(e1, t1[:, 0:n], t1[:, 0:n], nmr[:, 0:n], ALU.add, [t1k, "nmr"], [t1k])
                P.act(dst[:, oc, 0:n], t1[:, 0:n], AF.Identity, [t1k, "V"], [dstk],
                      bias=V[:, bcol + oc:bcol + oc + 1], scale=V[:, gcol + oc:gcol + oc + 1])
                P.cp("pool", dstb[:, oc, 0:n], dst[:, oc, 0:n], [dstk], [dstbk])

        last = []

        def tile(c0, n, halo, oc0):
            P.ld(bigb[:, 0:KC, 0:n], mTv[:, :, c0:c0 + n], "ldm", [], ["bigb"])
            P.ld(xres[:, :, 0:n], xTv[:, :, c0:c0 + n], "ldx", [], ["xres"])
            if not halo:
                P.ld(pst[:, :, 0:n], pTv[:, :, oc0:oc0 + n], "ldp", [], ["pst"])
                P.cp("pool", pb[:, :, 0:n], pst[:, :, 0:n], ["pst"], ["pb"])
            for og in range(4):
                wv_, wk_ = stream_w(w_out_v[:, :, og * 256:(og + 1) * 256], KC, 256)
                for o2 in range(2):
                    oc = og * 2 + o2
                    ps_, psk = ps_r.next()
                    for kc in range(KC):
                        P.mm(ps_[:, 0:n], wv_[:, kc, o2 * 128:(o2 + 1) * 128], bigb[:, kc, 0:n], kc == 0, kc == KC - 1,
                             [wk_, "bigb"], [psk])
                    P.stt("dve", xres[:, oc, 0:n], xres[:, oc, 0:n], ALPHA, ps_[:, 0:n], ALU.mult, ALU.add, [psk, "xres"], ["xres"])
            layernorm(xres, "xres", VEC_LN1G, VEC_LN1B, xm, "xm", actb, "actb", n)
            groups = [(0, 4), (4, 4), (8, 4), (12, 4), (16, 4), (20, 2)]
            for (f0, nf) in groups:
                if not halo:
                    wu, wuk = stream_w(w_up_v[:, :, f0 * 128:(f0 + nf) * 128], 8, nf * 128)
                wg, wgk = stream_w(w_up_v[:, :, FFN + f0 * 128:FFN + (f0 + nf) * 128], 8, nf * 128)
                for fi in range(nf):
                    fc = f0 + fi
                    hk = "ghalo%d" % fc
                    pg, pgk = ps_r.next()
                    for kc in range(8):
                        P.mm(pg[:, 0:n], wg[:, kc, fi * 128:(fi + 1) * 128], actb[:, kc, 0:n], kc == 0, kc == 7, [wgk, "actb"], [pgk])
                    if halo:
                        P.ts("dve", ghalo[:, fc, 0:2], pg[:, 0:2], hm[:, 0:1], None, ALU.mult, None, [pgk, "hm"], [hk])
                        continue
                    pu, puk = ps_r.next()
                    for kc in range(8):
                        P.mm(pu[:, 0:n], wu[:, kc, fi * 128:(fi + 1) * 128], actb[:, kc, 0:n], kc == 0, kc == 7, [wuk, "actb"], [puk])
                    u, uk = u_r.next()
                    P.cp("act", u[:, 0:n], pu[:, 0:n], [puk], [uk])
                    g, gk = g_r.next()
                    P.cp("act", g[:, 2:2 + n], pg[:, 0:n], [pgk], [gk])
                    P.cp("pool", g[:, 0:2], ghalo[:, fc, 0:2], [hk], [gk])
                    acc, acck = acc_r.next()
                    cw = VEC_CW + fc * 3
                    P.ts("dve", acc[:, 0:n], g[:, 2:2 + n], V[:, cw + 2:cw + 3], V[:, VEC_CB + fc:VEC_CB + fc + 1], ALU.mult, ALU.add,
                         [gk, "V"], [acck])
                    ctmp, ctmpk = tmp_r.next()
                    P.ts("pool", ctmp[:, 0:n], g[:, 1:1 + n], V[:, cw + 1:cw + 2], None, ALU.mult, None, [gk, "V"], [ctmpk])
                    P.stt("dve", acc[:, 0:n], g[:, 0:n], V[:, cw:cw + 1], acc[:, 0:n], ALU.mult, ALU.add, [gk, "V", acck], [acck])
                    P.tt("pool", acc[:, 0:n], acc[:, 0:n], ctmp[:, 0:n], ALU.add, [acck, ctmpk], [acck])
                    P.cp("pool", ghalo[:, fc, 0:2], g[:, n:n + 2], [gk], [hk])
                    hg, hgk = hg_r.next()
                    P.act(hg[:, 0:n], acc[:, 0:n], AF.Gelu, [acck], [hgk])
                    P.tt("dve", bigb[:, fc, 0:n], hg[:, 0:n], u[:, 0:n], ALU.mult, [hgk, uk], ["bigb"])
            if halo:
                return
            for oc in range(8):
                wd, wdk = stream_w(w_down_v[:, :, oc * 128:(oc + 1) * 128], NFC, 128)
                ps_, psk = ps_r.next()
                for fc in range(NFC):
                    P.mm(ps_[:, 0:n], wd[:, fc, :], bigb[:, fc, 0:n], fc == 0, fc == NFC - 1, [wdk, "bigb"], [psk])
                P.stt("dve", xm[:, oc, 0:n], xm[:, oc, 0:n], ALPHA, ps_[:, 0:n], ALU.mult, ALU.add, [psk, "xm"], ["xm"])
            layernorm(xm, "xm", VEC_LN2G, VEC_LN2B, xres, "xres", actb, "actb", n)
            wps, wpsk = ws_r.next()
            wpsv = wps[:, 0:2048].rearrange("p (a b) -> p a b", b=1024)
            P.ld(wpsv, w_proj_v[:, :, :], wpsk, [], [wpsk])
            P.cp("pool", wpb[:], wpsv, [wpsk], ["wpb"])
            wp, wpk = wpb, "wpb"
            for og in range(2):
                wgt, wgtk = stream_w(w_gate_v[:, :, og * 512:(og + 1) * 512], 8, 512)
                for o4 in range(4):
                    oc = og * 4 + o4
                    pg, pgk = ps_r.next()
                    for kc in range(8):
                        P.mm(pg[:, 0:n], wgt[:, kc, o4 * 128:(o4 + 1) * 128], actb[:, kc, 0:n], kc == 0, kc == 7, [wgtk, "actb"], [pgk])
                    gate, gatek = gate_r.next()
                    P.act(gate[:, 0:n], pg[:, 0:n], AF.Sigmoid, [pgk, "V"], [gatek], bias=V[:, VEC_BG + oc:VEC_BG + oc + 1])
                    pp, ppk = ps_r.next()
                    for kc in range(2):
                        P.mm(pp[:, 0:n], wp[:, kc, oc * 128:(oc + 1) * 128], pb[:, kc, 0:n], kc == 0, kc == 1, [wpk, "pb"], [ppk])
                    P.tt("dve", gate[:, 0:n], gate[:, 0:n], pp[:, 0:n], ALU.mult, [gatek, ppk], [gatek])
                    P.tt("pool", xres[:, oc, 0:n], xres[:, oc, 0:n], gate[:, 0:n], ALU.add, [gatek, "xres"], ["xres"])
                    P.cp("pool", outb[:, oc, 0:n], xres[:, oc, 0:n], ["xres"], ["outb"])
            last.append(P.ld(yTv[:, :, oc0:oc0 + n], xres[:, :, 0:n], "sty", ["xres"], []))
            last.append(P.ld(ybTv[:, :, oc0:oc0 + n], outb[:, :, 0:n], "styb", ["outb"], []))

        tile(0, 2, True, 0)
        for t in range(ntile):
            tile(2 + t * NT, NT, False, t * NT)
        P.emit(final_waits=last[-2:])
    return nc


def build_ssd(S):
    nc = bass.Bass("TRN2", target_bir_lowering=False)
    NT = min(512, S)
    ntile = S // NT
    nchunk_t = NT // 128
    x1b = nc.dram_tensor("x1b", [1024, S], BF16, kind="ExternalInput").ap()
    wz = nc.dram_tensor("wz", [1024, 512], F32, kind="ExternalInput").ap()
    wx = nc.dram_tensor("wx", [1024, 512], F32, kind="ExternalInput").ap()
    wB = nc.dram_tensor("wB", [1024, 256], F32, kind="ExternalInput").ap()
    wC = nc.dram_tensor("wC", [1024, 256], F32, kind="ExternalInput").ap()
    wdt = nc.dram_tensor("wdt", [1024, 8], F32, kind="ExternalInput").ap()
    cwd = nc.dram_tensor("cw", [128, 8 * 4], F32, kind="ExternalInput").ap()
    cbd = nc.dram_tensor("cb", [128, 8], F32, kind="ExternalInput").ap()
    dtbd = nc.dram_tensor("dtb", [1, 8], F32, kind="ExternalInput").ap()
    alogd = nc.dram_tensor("alog", [1, 8], F32, kind="ExternalInput").ap()
    drepd = nc.dram_tensor("drep", [1, 512], F32, kind="ExternalInput").ap()
    nwd = nc.dram_tensor("nw", [1, 512], F32, kind="ExternalInput").ap()
    yT = nc.dram_tensor("yT", [512, S], BF16, kind="ExternalOutput").ap()
    x1v = x1b.rearrange("(kc p) s -> p kc s", p=128)
    yTv = yT.rearrange("(c p) s -> p c s", p=128)
    P = Prog(nc)
    with contextlib.ExitStack() as st:
        C = Ctx(nc, st)
        ident, tri, upp, ones = _consts(P, C)
        wstage = C.sb("wstage", [128, 8, 512])
        identb = C.sb("identb", [128, 128], BF16); trib = C.sb("trib", [128, 128], BF16); uppb = C.sb("uppb", [128, 128], BF16)
        onesb = C.sb("onesb", [128, 128], BF16)
        P.cp("pool", identb[:], ident[:], ["ident"], ["identb"]); P.cp("pool", trib[:], tri[:], ["tri"], ["trib"])
        P.cp("pool", uppb[:], upp[:], ["upp"], ["uppb"]); P.cp("pool", onesb[:], ones[:], ["ones"], ["onesb"])
        ab_r = C.ring("ab", 2, [128, 2, 8], BF16)
        Wz = C.sb("Wz", [128, 8, 512], BF16); Wx = C.sb("Wx", [128, 8, 512], BF16)
        WB = C.sb("WB", [128, 8, 256], BF16); WC = C.sb("WC", [128, 8, 256], BF16); Wdt = C.sb("Wdt", [128, 8, 8], BF16)
        cw = C.sb("cw_sb", [128, 32]); cb = C.sb("cb_sb", [128, 8])
        dtb = C.sb("dtb_sb", [128, 8]); Abc = C.sb("Abc", [128, 8]); drep = C.sb("drep_sb", [128, 512]); nw = C.sb("nw_sb", [128, 512])
        stT = C.sb("stT", [128, 8, 64]); stTb = C.sb("stTb", [128, 8, 64], BF16)
        cbuf = C.sb("cbuf", [128, 8, NT + 3])
        xt_r = C.ring("xt", 2, [128, 8, NT], BF16)
        acc_r = C.ring("acc", 3, [128, NT]); ctmp_r = C.ring("ctmp", 2, [128, NT])
        sil = C.sb("silb", [128, 8, NT], BF16)
        zs_r = C.ring("zs", 2, [128, 512])
        sm_r = C.ring("sm", 2, [128, 8, 8])
        xtok_r = C.ring("xtok", 2, [128, 256]); Btok_r = C.ring("Btok", 2, [128, 128], BF16)
        cbm_r = C.ring("cbm", 2, [128, 128]); Lh_r = C.ring("Lh", 4, [128, 2, 128], BF16); eL_r = C.ring("eL", 2, [128, 512])
        MT_r = C.ring("MT", 3, [128, 128], BF16); xdt_r = C.ring("xdt", 3, [128, 64], BF16); xw_r = C.ring("xw", 3, [128, 64], BF16)
        ysb_r = C.ring("ysb", 2, [128, 256]); t1_r = C.ring("t1", 2, [128, 256]); yg_r = C.ring("yg", 2, [128, 256])
        sqj_r = C.ring("sqj", 2, [128, 256]); yn_r = C.ring("yn", 2, [128, 256], BF16)
        ss_r = C.ring("ss", 2, [128, 2])
        yTb_r = C.ring("yTb", 4, [128, 4, NT], BF16)
        banks = [C.ps("bank%d" % i, [128, 512]) for i in range(8) if i != 5]
        banks.insert(5, C.ps("bank_trb", [128, 1024], BF16))
        proj_r = Ring(banks[0:2], "psP"); arg_r = Ring(banks[2:4], "psG")
        b_m, b_tr, b_y, b_st = banks[4], banks[5], banks[6], banks[7]

        for (src, dst, key, ncol) in ((wz, Wz, "Wz", 512), (wx, Wx, "Wx", 512), (wB, WB, "WB", 256), (wC, WC, "WC", 256), (wdt, Wdt, "Wdt", 8)):
            P.ld(wstage[:, :, 0:ncol], src.rearrange("(kc p) n -> p kc n", p=128), "ldw", [], ["wstage"])
            P.cp("dve", dst[:], wstage[:, :, 0:ncol], ["wstage"], [key])
        P.ld(cw[:], cwd[:, :], "ldcw", [], ["cw"]); P.ld(cb[:], cbd[:, :], "ldcb", [], ["cb"])
        P.ld(dtb[:], dtbd[0:1, :].broadcast_to([128, 8]), "lddtb", [], ["dtb"])
        P.ld(Abc[:], alogd[0:1, :].broadcast_to([128, 8]), "ldabc", [], ["Abc"])
        P.ld(drep[:], drepd[0:1, :].broadcast_to([128, 512]), "lddrep", [], ["drep"])
        P.ld(nw[:], nwd[0:1, :].broadcast_to([128, 512]), "ldnw", [], ["nw"])
        P.act(Abc[:], Abc[:], AF.Exp, ["Abc"], ["Abc"])
        P.ts("dve", Abc[:], Abc[:], -1.0, None, ALU.mult, None, ["Abc"], ["Abc"])
        P.memset("pool", stT[:], 0.0, ["stT%d" % h for h in range(8)])
        P.memset("pool", stTb[:], 0.0, ["stTb%d" % h for h in range(8)])
        P.memset("pool", cbuf[:], 0.0, ["cbuf%d" % c for c in range(8)])
        last = []
        wsrc = [(Wx, "Wx", 0), (Wx, "Wx", 128), (Wx, "Wx", 256), (Wx, "Wx", 384), (WB, "WB", 0), (WB, "WB", 128), (WC, "WC", 0), (WC, "WC", 128)]
        for t in range(ntile):
            n = NT
            xt, xtk = xt_r.next()
            P.ld(xt[:], x1v[:, :, t * NT:(t + 1) * NT], xtk, [], [xtk])
            for ch in range(8):
                Wt, Wtk, cofs = wsrc[ch]
                pp, ppk = proj_r.next()
                for kc in range(8):
                    P.mm(pp[:, 0:n], Wt[:, kc, cofs:cofs + 128], xt[:, kc, :], kc == 0, kc == 7, [Wtk, xtk], [ppk])
                ck = "cbuf%d" % ch
                P.cp("act", cbuf[:, ch, 3:3 + n], pp[:, 0:n], [ppk], [ck])
                acc, acck = acc_r.next()
                P.ts("dve", acc[:, 0:n], cbuf[:, ch, 3:3 + n], cw[:, ch * 4 + 3:ch * 4 + 4], cb[:, ch:ch + 1], ALU.mult, ALU.add,
                     [ck, "cw", "cb"], [acck])
                for k in range(3):
                    if k == 1:
                        P.stt("dve", acc[:, 0:n], cbuf[:, ch, k:k + n], cw[:, ch * 4 + k:ch * 4 + k + 1], acc[:, 0:n], ALU.mult, ALU.add,
                              [ck, "cw", acck], [acck])
                    else:
                        ctmp, ctmpk = ctmp_r.next()
                        P.ts("pool", ctmp[:, 0:n], cbuf[:, ch, k:k + n], cw[:, ch * 4 + k:ch * 4 + k + 1], None, ALU.mult, None, [ck, "cw"], [ctmpk])
                        P.tt("pool", acc[:, 0:n], acc[:, 0:n], ctmp[:, 0:n], ALU.add, [acck, ctmpk], [acck])
                P.cp("pool", cbuf[:, ch, 0:3], cbuf[:, ch, n:n + 3], [ck], [ck])
                P.act(sil[:, ch, 0:n], acc[:, 0:n], AF.Silu, [acck], ["sil%d" % ch])
            yTb, yTbk = yTb_r.next()
            import os
            for q in range(nchunk_t if os.environ.get('SSD_SKIP') != 'chunks' else 0):
                tb = q * 128
                if t * nchunk_t + q >= int(os.environ.get('SSD_MAXCHUNK', '100000')):
                    continue
                sm, smk = sm_r.next()
                pz, pzk = proj_r.next()
                for kc in range(8):
                    P.mm(pz[:, :], xt[:, kc, tb:tb + 128], Wz[:, kc, :], kc == 0, kc == 7, ["Wz", xtk], [pzk])
                zs, zsk = zs_r.next()
                P.act(zs[:], pz[:, :], AF.Silu, [pzk], [zsk])
                for kc in range(8):
                    P.mm(b_m[:, 0:8], xt[:, kc, tb:tb + 128], Wdt[:, kc, :], kc == 0, kc == 7, ["Wdt", xtk], ["b_m"])
                P.tt("dve", sm[:, 0, :], b_m[:, 0:8], dtb[:], ALU.add, ["b_m", "dtb"], [smk])
                P.act(sm[:, 0, :], sm[:, 0, :], AF.Exp, [smk], [smk])
                P.act(sm[:, 1, :], sm[:, 0, :], AF.Ln, [smk], [smk], bias=1.0)
                P.tt("dve", sm[:, 2, :], sm[:, 1, :], Abc[:], ALU.mult, [smk, "Abc"], [smk])
                ab, abk = ab_r.next()
                P.cp("dve", ab[:, 0, :], sm[:, 2, :], [smk], [abk])
                P.cp("dve", sm[:, 7, :], ab[:, 0, :], [abk], [smk])
                P.tt("dve", sm[:, 5, :], sm[:, 2, :], sm[:, 7, :], ALU.subtract, [smk], [smk])
                P.cp("dve", ab[:, 1, :], sm[:, 5, :], [smk], [abk])
                P.cp("dve", sm[:, 2, :], sm[:, 5, :], [smk], [smk])
                P.mm(b_m[:, 8:16], trib[:], ab[:, 0, :], True, False, ["trib", abk], ["b_m"])
                P.mm(b_m[:, 8:16], trib[:], ab[:, 1, :], False, True, ["trib", abk], ["b_m"])
                P.mm(b_m[:, 16:24], onesb[:], ab[:, 0, :], True, False, ["onesb", abk], ["b_m"])
                P.mm(b_m[:, 16:24], onesb[:], ab[:, 1, :], False, True, ["onesb", abk], ["b_m"])
                P.act(sm[:, 3, :], b_m[:, 8:16], AF.Exp, ["b_m"], [smk])
                P.act(sm[:, 4, :], b_m[:, 16:24], AF.Exp, ["b_m"], [smk])
                P.cp("act", sm[:, 5, :], b_m[:, 8:16], ["b_m"], [smk])
                P.tt("dve", sm[:, 6, :], b_m[:, 16:24], sm[:, 5, :], ALU.subtract, ["b_m", smk], [smk])
                P.act(sm[:, 6, :], sm[:, 6, :], AF.Exp, [smk], [smk])
                P.tt("dve", sm[:, 6, :], sm[:, 6, :], sm[:, 1, :], ALU.mult, [smk], [smk])
                cbms = []
                for g in range(2):
                    P.mm(b_m[:, 128 + g * 128:256 + g * 128], sil[:, 4 + g, tb:tb + 128], sil[:, 6 + g, tb:tb + 128], True, True,
                         ["sil%d" % (4 + g), "sil%d" % (6 + g)], ["b_m"])
                for g in range(2):
                    cbm, cbmk = cbm_r.next()
                    P.tt("dve", cbm[:], b_m[:, 128 + g * 128:256 + g * 128], tri[:], ALU.mult, ["b_m", "tri"], [cbmk])
                    cbms.append((cbm, cbmk))
                LVL = int(os.environ.get('SSD_LEVEL', '9'))
                for g in range(2 if LVL >= 2 else 0):
                    cbm, cbmk = cbms[g]
                    for c2 in range(2):
                        P.tr(b_tr[:, c2 * 128:(c2 + 1) * 128], sil[:, 2 * g + c2, tb:tb + 128], identb[:], ["sil%d" % (2 * g + c2), "identb"], ["b_tr"])
                    P.tr(b_tr[:, 256:384], sil[:, 4 + g, tb:tb + 128], identb[:], ["sil%d" % (4 + g), "identb"], ["b_tr"])
                    xtok, xtokk = xtok_r.next()
                    if os.environ.get('SSD_E') != '1':
                        P.cp("act", xtok[:], b_tr[:, 0:256], ["b_tr"], [xtokk])
                    Btok, Btokk = Btok_r.next()
                    if os.environ.get('SSD_E') not in ('1', '2'):
                        P.cp("act", Btok[:], b_tr[:, 256:384], ["b_tr"], [Btokk])
                    if LVL < 3:
                        continue
                    pa, pak = arg_r.next()
                    for r in range(4):
                        hd = 4 * g + r
                        Lh, Lhk = Lh_r.next()
                        P.ts("pool", Lh[:, 0, :], uppb[:], sm[:, 7, hd:hd + 1], None, ALU.mult, None, ["uppb", smk], [Lhk])
                        P.ts("pool", Lh[:, 1, :], uppb[:], sm[:, 2, hd:hd + 1], None, ALU.mult, None, ["uppb", smk], [Lhk])
                        P.mm(pa[:, r * 128:(r + 1) * 128], Lh[:, 0, :], trib[:], True, False, [Lhk, "trib"], [pak])
                        P.mm(pa[:, r * 128:(r + 1) * 128], Lh[:, 1, :], trib[:], False, True, [Lhk, "trib"], [pak])
                    eL, eLk = eL_r.next()
                    P.act(eL[:], pa[:, :], AF.Exp, [pak], [eLk])
                    for r in range(4):
                        hd = 4 * g + r
                        MT, MTk = MT_r.next()
                        P.tt("pool", MT[:], cbm[:], eL[:, r * 128:(r + 1) * 128], ALU.mult, [cbmk, eLk], [MTk])
                        xdt, xdtk = xdt_r.next()
                        P.ts("dve", xdt[:], xtok[:, r * 64:(r + 1) * 64], sm[:, 1, hd:hd + 1], None, ALU.mult, None, [xtokk, smk], [xdtk])
                        xw, xwk = xw_r.next()
                        P.ts("pool", xw[:], xtok[:, r * 64:(r + 1) * 64], sm[:, 6, hd:hd + 1], None, ALU.mult, None, [xtokk, smk], [xwk])
                        P.mm(b_y[:, r * 64:(r + 1) * 64], MT[:], xdt[:], True, True, [MTk, xdtk], ["b_y"])
                        P.mm(b_y[:, 256 + r * 64:256 + (r + 1) * 64], sil[:, 6 + g, tb:tb + 128], stTb[:, hd, :], True, True,
                             ["sil%d" % (6 + g), "stTb%d" % hd], ["b_y"])
                        P.mm(b_st[:, r * 64:(r + 1) * 64], Btok[:], xw[:], True, True, [Btokk, xwk], ["b_st"])
                    if LVL < 4:
                        continue
                    ysb, ysbk = ysb_r.next()
                    P.cp("act", ysb[:], b_y[:, 0:256], ["b_y"], [ysbk])
                    for r in range(4):
                        hd = 4 * g + r
                        P.stt("dve", ysb[:, r * 64:(r + 1) * 64], b_y[:, 256 + r * 64:256 + (r + 1) * 64], sm[:, 3, hd:hd + 1],
                              ysb[:, r * 64:(r + 1) * 64], ALU.mult, ALU.add, ["b_y", smk, ysbk], [ysbk])
                        P.stt("dve", stT[:, hd, :], stT[:, hd, :], sm[:, 4, hd:hd + 1], b_st[:, r * 64:(r + 1) * 64], ALU.mult, ALU.add,
                              ["b_st", smk, "stT%d" % hd], ["stT%d" % hd])
                        P.cp("pool", stTb[:, hd, :], stT[:, hd, :], ["stT%d" % hd], ["stTb%d" % hd])
                    if LVL < 5:
                        continue
                    t1, t1k = t1_r.next()
                    P.tt("pool", t1[:], xtok[:], drep[:, g * 256:(g + 1) * 256], ALU.mult, [xtokk, "drep"], [t1k])
                    P.tt("pool", ysb[:], ysb[:], t1[:], ALU.add, [ysbk, t1k], [ysbk])
                    yg, ygk = yg_r.next()
                    P.tt("dve", yg[:], ysb[:], zs[:, g * 256:(g + 1) * 256], ALU.mult, [ysbk, zsk], [ygk])
                    sqj, sqjk = sqj_r.next(); ss, ssk = ss_r.next()
                    P.act(sqj[:], yg[:], AF.Square, [ygk], [sqjk])
                    P.op("dve", lambda e, ss=ss, sqj=sqj: e.reduce_sum(out=ss[:, 0:1], in_=sqj[:], axis=AX.X), [sqjk], [ssk])
                    P.act(ss[:, 1:2], ss[:, 0:1], AF.Sqrt, [ssk], [ssk], bias=RMS_EPS, scale=1.0 / 256)
                    P.op("dve", lambda e, ss=ss: e.reciprocal(out=ss[:, 0:1], in_=ss[:, 1:2]), [ssk], [ssk])
                    yn, ynk = yn_r.next()
                    P.stt("dve", yn[:], yg[:], ss[:, 0:1], nw[:, g * 256:(g + 1) * 256], ALU.mult, ALU.mult, [ygk, ssk, "nw"], [ynk])
                    for j in range(2):
                        P.tr(b_tr[:, 384 + j * 128:384 + (j + 1) * 128], yn[:, j * 128:(j + 1) * 128], identb[:], [ynk, "identb"], ["b_tr"])
                    for j in range(2):
                        P.cp("act", yTb[:, 2 * g + j, tb:tb + 128], b_tr[:, 384 + j * 128:384 + (j + 1) * 128], ["b_tr"], [yTbk])
            last.append(P.ld(yTv[:, :, t * NT:(t + 1) * NT], yTb[:], yTbk + "st", [yTbk], []))
        P.emit(final_waits=[("d", k, 16 * v) for k, v in P.dma_sems.items() if k.endswith("st")])
    return nc


_CACHE = {}


def _get(key, fn):
    if key not in _CACHE:
        _CACHE[key] = fn()
    return _CACHE[key]


def _c(a):
    return np.ascontiguousarray(a)


def _vecs(ln1g, ln1b, ln2g, ln2b, bg, convb, convw):
    v = np.zeros((128, NVEC), np.float32)
    v[:, VEC_LN1G:VEC_LN1G + 8] = ln1g.reshape(8, 128).T
    v[:, VEC_LN1B:VEC_LN1B + 8] = ln1b.reshape(8, 128).T
    v[:, VEC_LN2G:VEC_LN2G + 8] = ln2g.reshape(8, 128).T
    v[:, VEC_LN2B:VEC_LN2B + 8] = ln2b.reshape(8, 128).T
    v[:, VEC_BG:VEC_BG + 8] = bg.reshape(8, 128).T
    v[:, VEC_CB:VEC_CB + NFC] = convb.reshape(NFC, 128).T
    v[:, VEC_CW:VEC_CW + NFC * 3] = convw.reshape(3, NFC, 128).transpose(2, 1, 0).reshape(128, NFC * 3)
    return v


def run_post(mT_full, xT_full, pT_full, w_out, vecs, w_up, w_down, w_gate, w_proj, S, NT_POST):
    Bsz = len(mT_full)
    per = 8 // Bsz
    T = S // per
    Kin = mT_full[0].shape[0]
    NT = min(NT_POST, T)
    nc = _get(("post", Kin, T, NT), lambda: build_post(Kin, T, NT))
    in_maps = []
    for core in range(8):
        b, j = divmod(core, per)
        s0 = j * T
        m = np.zeros((Kin, 2 + T), mT_full[b].dtype); xx = np.zeros((1024, 2 + T), np.float32)
        if j > 0:
            m[:, 0:2] = mT_full[b][:, s0 - 2:s0]; xx[:, 0:2] = xT_full[b][:, s0 - 2:s0]
        m[:, 2:] = mT_full[b][:, s0:s0 + T]; xx[:, 2:] = xT_full[b][:, s0:s0 + T]
        in_maps.append({"mT": m, "xT": xx, "pT": _c(pT_full[b][:, s0:s0 + T]),
                        "hmask": np.full((128, 1), 1.0 if j > 0 else 0.0, np.float32), "vecs": vecs,
                        "w_out": _c(w_out), "w_up": _c(w_up), "w_down": _c(w_down), "w_gate": _c(w_gate), "w_proj": _c(w_proj)})
    res = run_bass_kernel_spmd(nc, in_maps, core_ids=list(range(8)))
    yT = [np.concatenate([np.asarray(res.results[b * per + j]["yT"]) for j in range(per)], axis=1) for b in range(Bsz)]
    ybT = [np.concatenate([np.asarray(res.results[b * per + j]["ybT"]) for j in range(per)], axis=1) for b in range(Bsz)]
    return yT, ybT


def run_attn(xT_full, attn_w_in, attn_b_f, S):
    Bsz = len(xT_full)
    per = 8 // Bsz
    NH = 16 // per
    nc = _get(("attn", S, NH), lambda: build_attn(S, NH))
    D = D_MODEL
    in_maps = []
    for core in range(8):
        b, j = divmod(core, per)
        cs = slice(j * NH * 64, (j + 1) * NH * 64)
        in_maps.append({"xT": xT_full[b],
                        "wq": _c(attn_w_in[:, 0:D][:, cs]), "wk": _c(attn_w_in[:, D:2 * D][:, cs]), "wv": _c(attn_w_in[:, 2 * D:3 * D][:, cs]),
                        "wf": _c(attn_w_in[:, 3 * D + j * NH:3 * D + (j + 1) * NH]), "bf": _c(attn_b_f[j * NH:(j + 1) * NH].reshape(1, NH))})
    res = run_bass_kernel_spmd(nc, in_maps, core_ids=list(range(8)))
    return [np.concatenate([np.asarray(res.results[b * per + j]["oT"]) for j in range(per)], axis=0) for b in range(Bsz)]


def run_ssd(x1b_full, w_in, conv_w, conv_b, dt_bias, A_log, Dsk, norm_w, S):
    Bsz = len(x1b_full)
    per = 8 // Bsz
    nc = _get(("ssd", S), lambda: build_ssd(S))
    DI = 2048
    in_maps = []
    for core in range(8):
        b, j = divmod(core, per)
        xs = slice(j * 512, (j + 1) * 512)
        gs = slice(j * 256, (j + 1) * 256)
        hs = slice(j * 8, (j + 1) * 8)
        wz = w_in[:, 0:DI][:, xs]
        wx = w_in[:, DI:2 * DI][:, xs]
        wB = w_in[:, 2 * DI:2 * DI + 1024][:, gs]
        wC = w_in[:, 2 * DI + 1024:2 * DI + 2048][:, gs]
        wdt = w_in[:, 2 * DI + 2048:][:, hs]
        cidx = np.concatenate([np.arange(j * 512, (j + 1) * 512), 2048 + np.arange(j * 256, (j + 1) * 256),
                               3072 + np.arange(j * 256, (j + 1) * 256)])
        cwl = conv_w[:, cidx]
        cw = cwl.reshape(4, 8, 128).transpose(2, 1, 0).reshape(128, 32)
        cb = conv_b[cidx].reshape(8, 128).T
        in_maps.append({"x1b": x1b_full[b], "wz": _c(wz), "wx": _c(wx), "wB": _c(wB), "wC": _c(wC), "wdt": _c(wdt),
                        "cw": _c(cw), "cb": _c(cb), "dtb": _c(dt_bias[hs].reshape(1, 8)), "alog": _c(A_log[hs].reshape(1, 8)),
                        "drep": _c(np.repeat(Dsk[hs], 64).reshape(1, 512)), "nw": _c(norm_w[xs].reshape(1, 512))})
    res = run_bass_kernel_spmd(nc, in_maps, core_ids=list(range(8)))
    return [np.concatenate([np.asarray(res.results[b * per + j]["yT"]) for j in range(per)], axis=0) for b in range(Bsz)]


NT_POST = 512


def kernel(x, p, attn_w_in, attn_b_f, attn_w_out, ssm_w_in, ssm_conv_w, ssm_conv_b, ssm_dt_bias, ssm_A_log, ssm_D,
           ssm_norm_w, ssm_w_out, ln_mix_g, ln_mix_b, ffn_w_up, ffn_conv_w, ffn_conv_b, ffn_w_down, ln_ffn_g, ln_ffn_b,
           ple_w_proj, ple_w_gate, ple_b_gate):
    x = np.asarray(x, np.float32); p = np.asarray(p, np.float32)
    Bsz, S, D = x.shape
    A = lambda a: np.asarray(a, np.float32)
    xT = [_c(x[b].T) for b in range(Bsz)]
    oT = run_attn(xT, A(attn_w_in)[0], A(attn_b_f)[0], S)
    v0 = _vecs(A(ln_mix_g)[0], A(ln_mix_b)[0], A(ln_ffn_g)[0], A(ln_ffn_b)[0], A(ple_b_gate)[0], A(ffn_conv_b)[0], A(ffn_conv_w)[0])
    pT0 = [_c(p[0, b].T) for b in range(Bsz)]
    x1T, x1bT = run_post(oT, xT, pT0, A(attn_w_out)[0], v0, A(ffn_w_up)[0], A(ffn_w_down)[0], A(ple_w_gate)[0], A(ple_w_proj)[0], S, NT_POST)
    yT = run_ssd(x1bT, A(ssm_w_in)[0], A(ssm_conv_w)[0], A(ssm_conv_b)[0], A(ssm_dt_bias)[0], A(ssm_A_log)[0], A(ssm_D)[0],
                 A(ssm_norm_w)[0], S)
    v1 = _vecs(A(ln_mix_g)[1], A(ln_mix_b)[1], A(ln_ffn_g)[1], A(ln_ffn_b)[1], A(ple_b_gate)[1], A(ffn_conv_b)[1], A(ffn_conv_w)[1])
    pT1 = [_c(p[1, b].T) for b in range(Bsz)]
    x2T, _ = run_post(yT, x1T, pT1, A(ssm_w_out)[0], v1, A(ffn_w_up)[1], A(ffn_w_down)[1], A(ple_w_gate)[1], A(ple_w_proj)[1], S, NT_POST)
    out = np.stack([_c(x2T[b].T) for b in range(Bsz)], axis=0).astype(np.float32)
    return out
```
